# Optimizing a Trainium2 kernel written in Bass

```python
import math
import jax
import jax.numpy as jnp
from jax import lax
import numpy as np

D_MODEL = 1024
BATCH = 8
SEQ = 4096
DEPTH = 4

CHUNK = 64

MIX_WIDTH = D_MODEL
POOL_WIDTH = MIX_WIDTH // 2
POOL_WINDOWS = (2, 4, 8, 16)
POOL_GROUPS = len(POOL_WINDOWS)
POOL_GROUP_DIM = POOL_WIDTH // POOL_GROUPS
HEAD_DIM = 64
ATTN_WIDTH = MIX_WIDTH - POOL_WIDTH
ATTN_HEADS = ATTN_WIDTH // HEAD_DIM

OFF_POOL_U = 0
OFF_POOL_G = OFF_POOL_U + POOL_WIDTH
OFF_Q = OFF_POOL_G + POOL_WIDTH
OFF_K = OFF_Q + ATTN_WIDTH
OFF_V = OFF_K + ATTN_WIDTH
OFF_ATTN_G = OFF_V + ATTN_WIDTH
OFF_F = OFF_ATTN_G + ATTN_WIDTH
IN_COLS = OFF_F + ATTN_HEADS

Q_BLOCK = 128
RMS_EPS = 1e-6
NEG_INF = -1e30

kernel_name = "hymba_pool_fox_trunk"


def rmsnorm(x, g):
    xf = x.astype(jnp.float32)
    y = xf * lax.rsqrt(jnp.mean(xf * xf, axis=-1, keepdims=True) + RMS_EPS)
    return (y * g.astype(jnp.float32)).astype(x.dtype)


def multi_scale_pool(u, pool_w, pool_scale):
    b, s, _ = u.shape
    uf = u.astype(jnp.float32)
    cs = jnp.concatenate([jnp.zeros((b, 1, POOL_WIDTH), jnp.float32), jnp.cumsum(uf, axis=1)], axis=1)
    t = jnp.arange(s)
    parts = []
    for gi, w in enumerate(POOL_WINDOWS):
        sl = slice(gi * POOL_GROUP_DIM, (gi + 1) * POOL_GROUP_DIM)
        c_g = cs[:, :, sl]
        lower = jnp.concatenate([jnp.zeros((b, w - 1, POOL_GROUP_DIM), jnp.float32), c_g[:, : s + 1 - w]], axis=1)
        count = jnp.minimum(t + 1, w).astype(jnp.float32)[None, :, None]
        parts.append((c_g[:, 1:] - lower) / count - uf[:, :, sl])
    d = jnp.stack(parts, axis=2)
    y = jnp.einsum('bsgc,gcd->bsgd', d, pool_w.astype(jnp.float32)).reshape(b, s, POOL_WIDTH)
    y = y * pool_scale.astype(jnp.float32)
    return y.astype(u.dtype)


def forgetting_attention(q, k, v, log_f):
    b, s, h, dh = q.shape
    q = q.transpose(0, 2, 1, 3)
    k = k.transpose(0, 2, 1, 3)
    v = v.transpose(0, 2, 1, 3)
    c = jnp.cumsum(log_f, axis=1).transpose(0, 2, 1)
    scale = 1.0 / math.sqrt(dh)
    outs = []
    for i in range(s // Q_BLOCK):
        q0, q1 = i * Q_BLOCK, (i + 1) * Q_BLOCK
        qb = q[:, :, q0:q1]
        kb = k[:, :, :q1]
        vb = v[:, :, :q1]
        logits = jnp.einsum('bhqd,bhkd->bhqk', qb, kb).astype(jnp.float32) * scale
        logits = logits + (c[:, :, q0:q1, None] - c[:, :, None, :q1])
        mask = jnp.arange(q0, q1)[:, None] >= jnp.arange(q1)[None, :]
        logits = jnp.where(mask[None, None], logits, NEG_INF)
        p = jax.nn.softmax(logits, axis=-1)
        outs.append(jnp.einsum('bhqk,bhkd->bhqd', p.astype(vb.dtype), vb))
    o = jnp.concatenate(outs, axis=2)
    return o.transpose(0, 2, 1, 3).reshape(b, s, h * dh)


def setup_inputs(seed: int = 0) -> dict:
    key = jax.random.key(seed)
    ks = jax.random.split(key, 9)
    x = jax.random.normal(ks[0], (BATCH, SEQ, D_MODEL), jnp.float32)
    norm_g = 1.0 + 0.02 * jax.random.normal(ks[1], (DEPTH, D_MODEL), jnp.float32)
    w_in = jax.random.normal(ks[2], (DEPTH, D_MODEL, IN_COLS), jnp.float32) * D_MODEL ** -0.5
    forget_bias = jax.random.uniform(ks[3], (DEPTH, ATTN_HEADS), jnp.float32, minval=1.0, maxval=3.0)
    pool_w = jax.random.normal(ks[4], (DEPTH, POOL_GROUPS, POOL_GROUP_DIM, POOL_GROUP_DIM), jnp.float32) * POOL_GROUP_DIM ** -0.5
    pool_scale = 1.0 + 0.02 * jax.random.normal(ks[5], (DEPTH, POOL_WIDTH), jnp.float32)
    w_out = jax.random.normal(ks[6], (DEPTH, MIX_WIDTH, D_MODEL), jnp.float32) * MIX_WIDTH ** -0.5
    final_g = 1.0 + 0.02 * jax.random.normal(ks[7], (D_MODEL,), jnp.float32)
    return {"x": x, "norm_g": norm_g, "w_in": w_in, "forget_bias": forget_bias,
            "pool_w": pool_w, "pool_scale": pool_scale, "w_out": w_out, "final_g": final_g}


def reference(x, norm_g, w_in, forget_bias, pool_w, pool_scale, w_out, final_g):
    b, s, _ = x.shape
    for layer in range(DEPTH):
        h = rmsnorm(x, norm_g[layer])
        proj = h @ w_in[layer]
        pool_u = proj[..., OFF_POOL_U:OFF_POOL_G]
        pool_g = proj[..., OFF_POOL_G:OFF_Q]
        q = proj[..., OFF_Q:OFF_K].reshape(b, s, ATTN_HEADS, HEAD_DIM)
        k = proj[..., OFF_K:OFF_V].reshape(b, s, ATTN_HEADS, HEAD_DIM)
        v = proj[..., OFF_V:OFF_ATTN_G].reshape(b, s, ATTN_HEADS, HEAD_DIM)
        attn_g = proj[..., OFF_ATTN_G:OFF_F]
        log_f = jax.nn.log_sigmoid(proj[..., OFF_F:IN_COLS].astype(jnp.float32)
                                   + forget_bias[layer].astype(jnp.float32))
        pool_out = multi_scale_pool(pool_u, pool_w[layer], pool_scale[layer]) * jax.nn.silu(pool_g)
        attn_out = forgetting_attention(q, k, v, log_f) * jax.nn.silu(attn_g)
        mixed = jnp.concatenate([pool_out, attn_out], axis=-1)
        x = x + mixed @ w_out[layer]
    return rmsnorm(x, final_g)
```

```python
import contextlib
import numpy as np
import ml_dtypes
import concourse.bass as bass
import concourse.mybir as mybir
from concourse.bass_utils import run_bass_kernel_spmd

F32 = mybir.dt.float32
BF16 = mybir.dt.bfloat16
AF = mybir.ActivationFunctionType
ALU = mybir.AluOpType

S = 4096
D = 1024
NT = 32
NQ = 8
H = 8
INC = 3080
OFF_PU, OFF_PG, OFF_Q, OFF_K, OFF_V, OFF_AG, OFF_F = 0, 512, 1024, 1536, 2048, 2560, 3072
DEPTH = 4
EPS = 1e-6
WCH = 1540

DEBUG = False
MODE = "fused"


class Prog:
    ENGS = ("pe", "act", "dve", "pool", "sp")

    def __init__(self):
        self.ops = {e: [] for e in self.ENGS}
        self.lastw = {}
        self.rds = {}
        self.dma_last = {}
        self.dma_cnt = {}

    def add(self, eng, fn, reads=(), writes=(), dma=None, extra_deps=()):
        deps = set(d for d in extra_deps if d is not None)
        for r in reads:
            t = self.lastw.get(r)
            if t is not None:
                deps.add(t)
        for w in writes:
            t = self.lastw.get(w)
            if t is not None:
                deps.add(t)
            for t in self.rds.get(w, {}).values():
                deps.add(t)
        idx = len(self.ops[eng])
        if dma is not None:
            n = self.dma_cnt.get(dma, 0) + 1
            self.dma_cnt[dma] = n
            tok = ("D", dma, n)
            prev = self.dma_last.get(dma)
            if prev is not None:
                deps.add(prev)
            self.dma_last[dma] = tok
        else:
            tok = ("E", eng, idx)
        best = {}
        for d in deps:
            k = (d[0], d[1])
            if k not in best or best[k][2] < d[2]:
                best[k] = d
        deps = set(best.values())
        self.ops[eng].append(dict(fn=fn, deps=deps, tok=tok, dma=dma, marked=False, semval=0))
        rk = (tok[0], tok[1])
        for r in reads:
            self.rds.setdefault(r, {})[rk] = tok
        for w in writes:
            self.lastw[w] = tok
            self.rds[w] = {}
        return tok

    def emit(self, nc, stack):
        for e in self.ENGS:
            for op in self.ops[e]:
                keep = set()
                for d in op["deps"]:
                    if d[0] == "E":
                        if d[1] == "pe" and e == "pe" and op["dma"] is None:
                            continue
                        self.ops[d[1]][d[2]]["marked"] = True
                    keep.add(d)
                op["deps"] = keep
        for e in self.ENGS:
            c = 0
            for op in self.ops[e]:
                if op["dma"] is None and op["marked"]:
                    c += 1
                    op["semval"] = c
        esem = {e: stack.enter_context(nc.semaphore("s_" + e)) for e in self.ENGS}
        dsem = {k: stack.enter_context(nc.semaphore("d_" + k)) for k in self.dma_cnt}

        def run(ename, eng):
            waited = {}
            for op in self.ops[ename]:
                for d in sorted(op["deps"]):
                    if d[0] == "E":
                        sem, key, val = esem[d[1]], "E" + d[1], self.ops[d[1]][d[2]]["semval"]
                    else:
                        sem, key, val = dsem[d[1]], "D" + d[1], 16 * d[2]
                    if waited.get(key, 0) < val:
                        eng.wait_ge(sem, val)
                        waited[key] = val
                ins = op["fn"](eng)
                if op["dma"] is not None:
                    ins.then_inc(dsem[op["dma"]], 16)
                elif op["marked"]:
                    ins.then_inc(esem[ename], 1)
            if ename == "sp":
                for k, n in self.dma_cnt.items():
                    if waited.get("D" + k, 0) < 16 * n:
                        eng.wait_ge(dsem[k], 16 * n)

        with nc.Block() as block:
            @block.tensor
            def _(e):
                run("pe", e)

            @block.scalar
            def _(e):
                run("act", e)

            @block.vector
            def _(e):
                run("dve", e)

            @block.gpsimd
            def _(e):
                run("pool", e)

            @block.sync
            def _(e):
                run("sp", e)


def build(nl, final):
    nc = bass.Bass("TRN2", target_bir_lowering=False)
    stack = contextlib.ExitStack()
    P = Prog()

    def din(name, shape, dt):
        return nc.dram_tensor(name, shape, dt, kind="ExternalInput").ap()

    def dscr(name, shape, dt):
        return nc.dram_tensor(name, shape, dt, kind="ExternalOutput" if DEBUG else "Internal").ap()

    x_in = din("x", [S, D], F32)
    w_in = din("w_in", [nl, D, INC], F32)
    w_out = din("w_out", [nl, D, D], F32)
    pool_w = din("pool_w", [nl, 4, 128, 128], F32)
    ng = din("ng", [128, nl, 8], F32)
    psc = din("psc", [128, nl, 4], F32)
    fb = din("fb", [128, nl, 8], F32)
    fg = din("fg", [128, D], F32)
    c_identb = din("identb", [128, 128], BF16)
    c_identf = din("identf", [128, 128], F32)
    c_trif = din("trif", [128, 128], F32)
    c_onesf = din("onesf", [128, 128], F32)
    c_maskn = din("maskn", [128, 128], BF16)
    c_invc = din("invc", [128, 4, 16], F32)
    out = nc.dram_tensor("out", [S, D], F32, kind="ExternalOutput").ap()

    xa = dscr("xa", [S, D], F32)
    xb = dscr("xb", [S, D], F32)
    qT = dscr("qT", [512, S], BF16)
    kT = dscr("kT", [512, S], BF16)
    gT = dscr("gT", [512, S], F32)
    mT = dscr("mT", [1024, S], BF16)
    crow = dscr("crow", [NT, H, 128], BF16)

    def sb(name, shape, dt):
        return stack.enter_context(nc.sbuf_tensor(name, shape, dt))

    w_in_sb = sb("w_in_sb", [128, 8, INC], BF16)
    w_out_sb = sb("w_out_sb", [128, 8, D], BF16)
    poolw_sb = sb("poolw_sb", [128, 4, 128], BF16)
    V_flat = sb("V_all", [128, NT * H * 65 + 64], BF16)
    V_all = V_flat[:, 0:NT * H * 65].rearrange("p (a b c) -> p a b c", a=NT, b=H)
    Call = sb("Call", [128, NT, H], F32)
    identb = sb("identb_sb", [128, 128], BF16)
    identf = sb("identf_sb", [128, 128], F32)
    trif = sb("trif_sb", [128, 128], F32)
    onesf = sb("onesf_sb", [128, 128], F32)
    maskn = sb("maskn_sb", [128, 128], BF16)
    invc = sb("invc_sb", [128, 4, 16], F32)
    ng_sb = sb("ng_sb", [128, nl, 8], F32)
    psc_sb = sb("psc_sb", [128, nl, 4], F32)
    fb_sb = sb("fb_sb", [128, nl, 8], F32)
    fg_sb = sb("fg_sb", [128, D], F32)
    stage = sb("stage", [128, WCH], F32)
    big = [sb("big%d" % i, [128, 4096], BF16) for i in range(2)]
    xt = [sb("xt%d" % i, [128, D], F32) for i in range(4)]
    hb = [sb("hb%d" % i, [128, D], BF16) for i in range(2)]
    ubuf = sb("ubuf", [128, 4, 528], F32)
    ptmp = [sb("ptmp%d" % i, [128, 528], F32) for i in range(2)]
    dd = [sb("dd%d" % i, [128, 512], BF16) for i in range(2)]
    ge = [sb("ge%d" % i, [128, 512], F32) for i in range(2)]
    sg = [sb("sg%d" % i, [128, 512], F32) for i in range(2)]
    qk_sb = [sb("qk%d" % i, [128, 512], BF16) for i in range(3)]
    mp_sb = [sb("mp%d" % i, [128, 512], BF16) for i in range(2)]
    st1 = [sb("st1_%d" % i, [128, 4], F32) for i in range(2)]
    fz = [sb("fz%d" % i, [128, 3, 8], F32) for i in range(2)]
    carry = [sb("carry%d" % i, [128, 8], F32) for i in range(2)]
    cneg = sb("cneg", [128, 4, 8], F32)
    crT = sb("crT", [32, 128], BF16)
    QTa = [sb("QTa%d" % i, [128, 512], BF16) for i in range(2)]
    ga = [sb("ga%d" % i, [64, 512], F32) for i in range(4)]
    Pt = [sb("Pt%d" % i, [128, 512], BF16) for i in range(4)]
    Tsb = [sb("Tsb%d" % i, [65, 512], F32) for i in range(2)]
    srow = [sb("srow%d" % i, [128, 1024], BF16) for i in range(2)]
    sel = sb("sel", [128, 128], BF16)
    msb = [sb("msb%d" % i, [64, 512], BF16) for i in range(2)]
    ma = [sb("ma%d" % i, [128, 8, 128], BF16) for i in range(3)]

    ps = [stack.enter_context(nc.psum_tensor("ps%d" % i, [128, 512], F32)) for i in range(8)]

    rr = {"b": 0, "st": 0}

    def nextbank(n=8):
        b = rr["b"] % n
        rr["b"] += 1
        return b

    def stkey():
        rr["st"] += 1
        return "st%d" % (rr["st"] % 8)

    def dma(out_ap, in_ap, reads, writes, key):
        q = "pool" if (key.startswith("st") or key.startswith("xs")) else "sp"
        P.add(q, lambda e: e.dma_start(out=out_ap, in_=in_ap), reads=reads, writes=writes, dma=key)

    for i, (dst, src) in enumerate([(identb, c_identb), (identf, c_identf), (trif, c_trif), (onesf, c_onesf),
                                    (maskn, c_maskn), (invc, c_invc), (ng_sb, ng), (psc_sb, psc),
                                    (fb_sb, fb), (fg_sb, fg)]):
        dma(dst[:], src, [], [("c", i)], "c%d" % (i % 2))
    CONST = [("c", i) for i in range(10)]
    P.add("pool", lambda e: e.memset(V_flat[:, NT * H * 65:], 0.0), writes=[("V", kt) for kt in range(NT)])
    P.add("pool", lambda e: e.memset(V_all[:, :, :, 64:65], 1.0), writes=[("V", kt) for kt in range(NT)])
    P.add("pool", lambda e: e.memset(sel[:], 0.0), writes=["sel"])
    P.add("pool", lambda e: e.memset(sel[64:65, 0:64], 1.0), writes=["sel"])
    for i in range(2):
        P.add("pool", lambda e, i=i: e.memset(srow[i][:], 0.0), writes=[("srow", i)])

    Xbuf = [x_in] + [xa if (l % 2 == 1) else xb for l in range(1, nl)] + [out]
    Xres = ["xin"] + ["xa" if (l % 2 == 1) else "xb" for l in range(1, nl)] + ["xout"]

    def load_w_in(l):
        steps = []
        for kc in range(8):
            for cc in range(INC // WCH):
                c0 = cc * WCH

                def d(kc=kc, c0=c0):
                    dma(stage[:, 0:WCH], w_in[l, kc * 128:(kc + 1) * 128, c0:c0 + WCH], [], ["stage"], "wst")

                def c(kc=kc, c0=c0):
                    P.add("dve", lambda e: e.tensor_scalar(w_in_sb[:, kc, c0:c0 + WCH], stage[:, 0:WCH],
                                                           ng_sb[:, l, kc:kc + 1], None, ALU.mult),
                          reads=["stage"] + CONST, writes=["w_in_sb"])
                steps.append((d, c))
        return steps

    def load_w_out(l):
        steps = []
        for kc in range(8):
            def d(kc=kc):
                dma(stage[:, 0:D], w_out[l, kc * 128:(kc + 1) * 128, :], [], ["stage"], "wst")

            def c(kc=kc):
                P.add("dve", lambda e: e.tensor_copy(w_out_sb[:, kc, :], stage[:, 0:D]),
                      reads=["stage"], writes=["w_out_sb"])
            steps.append((d, c))
        return steps

    def load_pool_w(l):
        def d():
            dma(stage[:, 0:512].rearrange("p (g c) -> p g c", g=4), pool_w[l].rearrange("g p c -> p g c"),
                [], ["stage"], "wst")

        def c():
            P.add("dve", lambda e: e.tensor_copy(poolw_sb[:].rearrange("p g c -> p (g c)"), stage[:, 0:512]),
                  reads=["stage"], writes=["poolw_sb"])
        return [(d, c)]

    def phase1_front(l, j):
        steps = []
        slot = j % 2
        hT = big[slot]
        xsrc = Xbuf[l]

        for sub in range(4):
            kt = j * 4 + sub
            xs = kt % 4
            hs = kt % 2

            def s_norm(kt=kt, xs=xs, hs=hs):
                if l == 0:
                    x0_load(kt)
                    x0_load(kt + 1)
                    x0_load(kt + 2)
                else:
                    p3_tile(l - 1, kt)
                s = st1[hs]
                P.add("act", lambda e: e.activation(hb[hs][:], xt[xs][:], AF.Square, accum_out=s[:, 0:1]),
                      reads=[("xt", xs)], writes=[("hb", hs), ("st1", hs)])
                P.add("act", lambda e: e.activation(s[:, 1:2], s[:, 0:1], AF.Ln, bias=EPS, scale=1.0 / D),
                      reads=[("st1", hs)], writes=[("st1", hs)])
                P.add("act", lambda e: e.activation(s[:, 2:3], s[:, 1:2], AF.Exp, scale=-0.5),
                      reads=[("st1", hs)], writes=[("st1", hs)])
                P.add("act", lambda e: e.activation(hb[hs][:], xt[xs][:], AF.Copy, scale=s[:, 2:3]),
                      reads=[("xt", xs), ("st1", hs)], writes=[("hb", hs)])
            steps.append(s_norm)

            def s_tr(kt=kt, sub=sub, hs=hs):
                for half in range(2):
                    b = nextbank()
                    for q in range(4):
                        kc = half * 4 + q
                        P.add("pe", lambda e, kc=kc, q=q, b=b: e.matmul(
                            ps[b][:, q * 128:(q + 1) * 128], lhsT=hb[hs][:, kc * 128:(kc + 1) * 128],
                            rhs=identb[:], start=True, stop=True),
                            reads=[("hb", hs)] + CONST, writes=[("ps", b)])
                    P.add("dve", lambda e, b=b, half=half: e.tensor_copy(
                        hT[:].rearrange("p (k t) -> p k t", k=8)[:, half * 4:half * 4 + 4, sub * 128:(sub + 1) * 128],
                        ps[b][:].rearrange("p (k t) -> p k t", k=4)),
                        reads=[("ps", b)], writes=[("big", slot, sub)])
            steps.append(s_tr)

            def s_vf(kt=kt, sub=sub):
                hT3 = hT[:].rearrange("p (k t) -> p k t", k=8)
                b = nextbank()
                for kc in range(8):
                    P.add("pe", lambda e, kc=kc, b=b: e.matmul(
                        ps[b][:, :], lhsT=hT3[:, kc, sub * 128:(sub + 1) * 128],
                        rhs=w_in_sb[:, kc, OFF_V:OFF_V + 512], start=(kc == 0), stop=(kc == 7)),
                        reads=[("big", slot, sub), "w_in_sb"], writes=[("ps", b)])
                P.add("act", lambda e, b=b: e.activation(
                    V_all[:, kt, :, 0:64], ps[b][:].rearrange("p (h d) -> p h d", h=8), AF.Copy),
                    reads=[("ps", b)], writes=[("V", kt)])
                b2 = nextbank()
                for kc in range(8):
                    P.add("pe", lambda e, kc=kc, b2=b2: e.matmul(
                        ps[b2][:, 0:8], lhsT=hT3[:, kc, sub * 128:(sub + 1) * 128],
                        rhs=w_in_sb[:, kc, OFF_F:OFF_F + 8], start=(kc == 0), stop=(kc == 7)),
                        reads=[("big", slot, sub), "w_in_sb"], writes=[("ps", b2)])
                f = fz[kt % 2]
                fr = ("fz", kt % 2)
                P.add("dve", lambda e: e.tensor_tensor(f[:, 0, :], ps[b2][:, 0:8], fb_sb[:, l, :], ALU.add),
                      reads=[("ps", b2)] + CONST, writes=[fr])
                P.add("act", lambda e: e.activation(f[:, 1, :], f[:, 0, :], AF.Exp, scale=-1.0),
                      reads=[fr], writes=[fr])
                P.add("act", lambda e: e.activation(f[:, 2, :], f[:, 1, :], AF.Ln, bias=1.0, scale=1.0),
                      reads=[fr], writes=[fr])
            steps.append(s_vf)

            def s_cum(kt=kt, sub=sub):
                f = fz[kt % 2]
                fr = ("fz", kt % 2)
                b3 = nextbank()
                P.add("pe", lambda e: e.matmul(ps[b3][:, 0:8], lhsT=trif[:], rhs=f[:, 2, :], start=True, stop=True),
                      reads=[fr] + CONST, writes=[("ps", b3)])
                P.add("pe", lambda e: e.matmul(ps[b3][:, 8:16], lhsT=onesf[:], rhs=f[:, 2, :], start=True, stop=True),
                      reads=[fr] + CONST, writes=[("ps", b3)])
                cin, cout = carry[kt % 2], carry[(kt + 1) % 2]
                if kt == 0:
                    P.add("dve", lambda e: e.tensor_copy(Call[:, kt, :], ps[b3][:, 0:8]),
                          reads=[("ps", b3)], writes=[("C", kt)])
                    P.add("dve", lambda e: e.tensor_copy(cout[:], ps[b3][:, 8:16]),
                          reads=[("ps", b3)], writes=[("carry", (kt + 1) % 2)])
                else:
                    P.add("dve", lambda e: e.tensor_tensor(Call[:, kt, :], ps[b3][:, 0:8], cin[:], ALU.add),
                          reads=[("ps", b3), ("carry", kt % 2)], writes=[("C", kt)])
                    P.add("dve", lambda e: e.tensor_tensor(cout[:], ps[b3][:, 8:16], cin[:], ALU.add),
                          reads=[("ps", b3), ("carry", kt % 2)], writes=[("carry", (kt + 1) % 2)])
                P.add("dve", lambda e: e.tensor_scalar(cneg[:, sub, :], Call[:, kt, :], -8.0, None, ALU.mult),
                      reads=[("C", kt)], writes=["cneg"])
            steps.append(s_cum)

        def s_crow():
            b = nextbank()
            P.add("pe", lambda e: e.transpose(ps[b][0:32, 0:128], cneg[:].rearrange("p s h -> p (s h)"), identf[:]),
                  reads=["cneg"] + CONST, writes=[("ps", b)])
            P.add("act", lambda e: e.activation(crT[:], ps[b][0:32, 0:128], AF.Copy),
                  reads=[("ps", b)], writes=["crT"])
            dma(crow[j * 4:(j + 1) * 4].rearrange("s h t -> (s h) t"), crT[:], ["crT"], [("cr", j)], stkey())
        steps.append(s_crow)
        n_, t_, v_, c_ = ([steps[4 * i + k] for i in range(4)] for k in range(4))
        order = [n_[0], n_[1], t_[0], n_[2], t_[1], v_[0], n_[3], t_[2], v_[1], c_[0], t_[3], v_[2], c_[1],
                 v_[3], c_[2], c_[3]]
        return order, steps[16]

    def phase1_fm(l, j):
        steps = []
        slot = j % 2
        hT3 = big[slot][:].rearrange("p (k t) -> p k t", k=8)
        hres = [("big", slot, s) for s in range(4)]
        cols = slice(j * 512, (j + 1) * 512)

        def proj(c0):
            b = nextbank()
            for kc in range(8):
                P.add("pe", lambda e, kc=kc: e.matmul(
                    ps[b][:, :], lhsT=w_in_sb[:, kc, c0:c0 + 128], rhs=hT3[:, kc, :],
                    start=(kc == 0), stop=(kc == 7)),
                    reads=hres + ["w_in_sb"], writes=[("ps", b)])
            return b

        def gate(b, gs):
            P.add("act", lambda e: e.activation(ge[gs][:], ps[b][:, :], AF.Exp, scale=-1.0),
                  reads=[("ps", b)], writes=[("ge", gs)])
            P.add("act", lambda e: e.activation(ge[gs][:], ge[gs][:], AF.Ln, bias=1.0, scale=1.0),
                  reads=[("ge", gs)], writes=[("ge", gs)])
            P.add("act", lambda e: e.activation(ge[gs][:], ge[gs][:], AF.Exp, scale=-1.0),
                  reads=[("ge", gs)], writes=[("ge", gs)])
            P.add("dve", lambda e: e.tensor_tensor(sg[gs][:], ps[b][:, :], ge[gs][:], ALU.mult),
                  reads=[("ps", b), ("ge", gs)], writes=[("sg", gs)])

        pa, pb, qs_ = [], [], []
        for g in range(4):
            def s_pool(g=g):
                w = 2 << g
                gs = g % 2
                ur = ("u", g)
                if j == 0:
                    P.add("pool", lambda e: e.memset(ubuf[:, g, 0:16], 0.0), writes=[ur])
                b = proj(OFF_PU + g * 128)
                P.add("act", lambda e: e.activation(ubuf[:, g, 16:528], ps[b][:, :], AF.Copy),
                      reads=[("ps", b)], writes=[ur])
                b2 = proj(OFF_PG + g * 128)
                gate(b2, gs)
                src = ubuf[:, g, :]
                srcr = ur
                sh = 1
                k = 0
                while sh < w:
                    dst = ptmp[k % 2]
                    dr = ("ptmp", k % 2)
                    lo = 2 * sh - 1
                    P.add("dve", lambda e, src=src, dst=dst, sh=sh, lo=lo: e.tensor_tensor(
                        dst[:, lo:528], src[:, lo:528], src[:, lo - sh:528 - sh], ALU.add),
                        reads=[srcr], writes=[dr])
                    src, srcr = dst[:, :], dr
                    sh *= 2
                    k += 1
                ds = ("dd", gs)
                P.add("dve", lambda e, src=src: e.scalar_tensor_tensor(
                    dd[gs][:], src[:, 16:528], 1.0 / w, ubuf[:, g, 16:528], ALU.mult, ALU.subtract),
                    reads=[srcr, ur], writes=[ds])
                if j == 0:
                    P.add("dve", lambda e, src=src: e.tensor_tensor(
                        ptmp[k % 2][:, 0:16], src[:, 16:32], invc[:, g, :], ALU.mult),
                        reads=[srcr] + CONST, writes=[("ptmp", k % 2)])
                    P.add("dve", lambda e: e.tensor_tensor(
                        dd[gs][:, 0:16], ptmp[k % 2][:, 0:16], ubuf[:, g, 16:32], ALU.subtract),
                        reads=[("ptmp", k % 2), ur], writes=[ds])
                P.add("dve", lambda e: e.tensor_copy(ubuf[:, g, 0:16], ubuf[:, g, 512:528]),
                      reads=[ur, ("ptmp", 0), ("ptmp", 1), ds], writes=[ur])
            pa.append(s_pool)

            def s_poolb(g=g):
                gs = g % 2
                ds = ("dd", gs)
                b3 = nextbank()
                P.add("pe", lambda e: e.matmul(ps[b3][:, :], lhsT=poolw_sb[:, g, :], rhs=dd[gs][:],
                                               start=True, stop=True),
                      reads=[ds, "poolw_sb"], writes=[("ps", b3)])
                P.add("dve", lambda e: e.scalar_tensor_tensor(
                    mp_sb[gs][:], ps[b3][:, :], psc_sb[:, l, g:g + 1], sg[gs][:], ALU.mult, ALU.mult),
                    reads=[("ps", b3), ("sg", gs)] + CONST, writes=[("mp", gs)])
                dma(mT[g * 128:(g + 1) * 128, cols], mp_sb[gs][:], [("mp", gs)], [("mT", g, j)], stkey())
            pb.append(s_poolb)

        for p in range(4):
            def s_q(p=p):
                for which, off, dst, nm in ((0, OFF_Q, qT, "qT"), (1, OFF_K, kT, "kT")):
                    b = proj(off + p * 128)
                    qs = (2 * p + which) % 3
                    if which == 0:
                        P.add("dve", lambda e, b=b, qs=qs: e.tensor_copy(qk_sb[qs][:], ps[b][:, :]),
                              reads=[("ps", b)], writes=[("qk", qs)])
                    else:
                        P.add("act", lambda e, b=b, qs=qs: e.activation(qk_sb[qs][:], ps[b][:, :], AF.Copy),
                              reads=[("ps", b)], writes=[("qk", qs)])
                    dma(dst[p * 128:(p + 1) * 128, cols], qk_sb[qs][:], [("qk", qs)], [(nm, p, j)], stkey())
                b = proj(OFF_AG + p * 128)
                gs = p % 2
                gate(b, gs)
                dma(gT[p * 128:(p + 1) * 128, cols], sg[gs][:], [("sg", gs)], [("gT", p, j)], stkey())
            qs_.append(s_q)
        return [pa[0], pa[1], pb[0], pa[2], pb[1], pa[3], pb[2], qs_[0], pb[3], qs_[1], qs_[2], qs_[3]]

    def phase2(l, extra_steps):
        extra = list(extra_steps)
        blocks = []
        for h in range(H):
            for j in range(NQ):
                for kt in range(4 * j + 4):
                    blocks.append((h, j, kt))
        LOOK = 3
        DEFER = 5
        pending = []
        exp_tok = {}
        castq = []

        def load_k(h):
            ks = h % 2
            kres = [("big", ks, s) for s in range(4)]
            dma(big[ks][0:64, :], kT[h * 64:(h + 1) * 64, :], [("kT", h // 2, jj) for jj in range(NQ)], kres,
                "kl%d" % ks)
            P.add("pool", lambda e: e.memset(big[ks][64:65, :], 1.0), writes=kres)

        def load_q(h, j):
            qi = h * NQ + j
            qs = qi % 2
            cols = slice(j * 512, (j + 1) * 512)
            dma(QTa[qs][0:64, :], qT[h * 64:(h + 1) * 64, cols], [("qT", h // 2, j)], [("QTa", qs)], "ql%d" % qs)
            dma(QTa[qs][64:65, :].rearrange("p (s t) -> p s t", s=4),
                crow[j * 4:(j + 1) * 4, h:h + 1, :].rearrange("s o t -> o s t"),
                [("cr", j)], [("QTc", qs)], "cl%d" % qs)
            gs3 = qi % 4
            dma(ga[gs3][:], gT[h * 64:(h + 1) * 64, cols], [("gT", h // 2, j)], [("ga", gs3)], "gl%d" % gs3)

        def emit_S(i):
            h, j, kt = blocks[i]
            qs = (h * NQ + j) % 2
            ks = h % 2
            b = i % 4
            diag = kt >= 4 * j
            c0 = 128 * (kt - 4 * j) if diag else 0
            P.add("pe", lambda e: e.matmul(ps[b][:, c0:512], lhsT=big[ks][0:65, kt * 128:(kt + 1) * 128],
                                           rhs=QTa[qs][0:65, c0:512], start=True, stop=not diag),
                  reads=[("big", ks, 0), ("QTa", qs), ("QTc", qs)], writes=[("ps", b)],
                  extra_deps=[exp_tok.get(i - LOOK)])
            if diag:
                P.add("pe", lambda e: e.matmul(ps[b][:, c0:c0 + 128], lhsT=identb[:], rhs=maskn[:],
                                               start=False, stop=True),
                      reads=CONST, writes=[("ps", b)])
            exp_tok[i] = P.add("act", lambda e: e.activation(Pt[b][:, c0:512], ps[b][:, c0:512], AF.Exp,
                                                             bias=Call[:, kt, h:h + 1], scale=0.125),
                               reads=[("ps", b), ("C", kt)], writes=[("Pt", b)])
            if DEBUG and i == 0:
                dS = nc.dram_tensor("dbgS", [128, 512], F32, kind="ExternalOutput").ap()
                dP = nc.dram_tensor("dbgP", [128, 512], BF16, kind="ExternalOutput").ap()
                dK = nc.dram_tensor("dbgK", [65, 512], BF16, kind="ExternalOutput").ap()
                dQ = nc.dram_tensor("dbgQ", [65, 512], BF16, kind="ExternalOutput").ap()
                P.add("act", lambda e: e.activation(xt[0][:, 0:512], ps[b][:, :], AF.Copy),
                      reads=[("ps", b)], writes=[("xt", 0)])
                dma(dS, xt[0][:, 0:512], [("xt", 0)], ["dbgS"], "c0")
                dma(dP, Pt[b][:, :], [("Pt", b)], ["dbgP"], "c0")
                dma(dK, big[ks][0:65, 0:512], [("big", ks, 0)], ["dbgK"], "c0")
                dma(dQ, QTa[qs][0:65, :], [("QTa", qs), ("QTc", qs)], ["dbgQ"], "c0")

        def emit_PV(i):
            h, j, kt = blocks[i]
            b = i % 4
            qi = h * NQ + j
            ob = (4, 5, 7)[qi % 3]
            qs = qi % 2
            diag = kt >= 4 * j
            c0 = 128 * (kt - 4 * j) if diag else 0
            last = kt == 4 * j + 3
            v0 = (kt * H + h) * 65
            P.add("pe", lambda e: e.matmul(ps[ob][:, c0:512], lhsT=V_flat[:, v0:v0 + 128], rhs=Pt[b][:, c0:512],
                                           start=(kt == 0), stop=last),
                  reads=[("Pt", b), ("V", kt), ("V", min(kt + 1, NT - 1))], writes=[("ps", ob)])
            if last:
                cols = slice(j * 512, (j + 1) * 512)
                es = qi % 2
                T_, sr_ = Tsb[es], srow[es]
                P.add("dve", lambda e: e.tensor_copy(sr_[64:65, 0:512], ps[ob][64:65, :]),
                      reads=[("ps", ob)], writes=[("srow", es)])
                P.add("dve", lambda e: e.tensor_tensor(sr_[64:65, 512:1024], ps[ob][64:65, :], sr_[64:65, 0:512],
                                                       ALU.subtract),
                      reads=[("ps", ob), ("srow", es)], writes=[("srow", es)])

                def part2():
                    P.add("pe", lambda e: e.matmul(ps[6][:, :], lhsT=sel[:], rhs=sr_[:, 0:512],
                                                   start=True, stop=False),
                          reads=[("srow", es), "sel"], writes=[("ps", 6)])
                    P.add("pe", lambda e: e.matmul(ps[6][:, :], lhsT=sel[:], rhs=sr_[:, 512:1024],
                                                   start=False, stop=True),
                          reads=[("srow", es), "sel"], writes=[("ps", 6)])
                    gs3 = qi % 4
                    P.add("dve", lambda e: e.reciprocal(T_[0:64, :], ps[6][0:64, :]),
                          reads=[("ps", 6)], writes=[("Tsb", es)])
                    P.add("dve", lambda e: e.tensor_tensor(T_[0:64, :], T_[0:64, :], ga[gs3][:], ALU.mult),
                          reads=[("Tsb", es), ("ga", gs3)], writes=[("Tsb", es)])
                    P.add("dve", lambda e: e.tensor_tensor(msb[qs][:], ps[ob][0:64, :], T_[0:64, :], ALU.mult),
                          reads=[("ps", ob), ("Tsb", es)], writes=[("msb", qs)])
                    dma(mT[512 + h * 64:512 + (h + 1) * 64, cols], msb[qs][:], [("msb", qs)],
                        [("mT", 4 + h // 2, j)], stkey())
                pending.append((i + LOOK + DEFER, part2))

        load_k(0)
        load_q(0, 0)
        n = len(blocks)
        for i in range(n + LOOK):
            while pending and pending[0][0] <= i:
                pending.pop(0)[1]()
            if i < n:
                h, j, kt = blocks[i]
                if kt == 0:
                    nxt = h * NQ + j + 1
                    if nxt < H * NQ:
                        nh, nj = divmod(nxt, NQ)
                        if nj == 0:
                            load_k(nh)
                        load_q(nh, nj)
                    if j >= 2:
                        if castq:
                            castq.pop(0)()
                        if extra:
                            d_, c_ = extra.pop(0)
                            d_()
                            castq.append(c_)
                emit_S(i)
            if i - LOOK >= 0:
                emit_PV(i - LOOK)
        while pending:
            pending.pop(0)[1]()
        while castq or extra:
            if castq:
                castq.pop(0)()
            if extra:
                d_, c_ = extra.pop(0)
                d_()
                castq.append(c_)

    loaded = set()

    def p3_loads(l, kt, with_ma=True):
        if kt >= NT:
            return
        xs, ms, j = kt % 4, kt % 3, kt // 4
        if (l, kt) not in loaded:
            loaded.add((l, kt))
            dma(xt[xs][:], Xbuf[l][kt * 128:(kt + 1) * 128, :], [("x", Xres[l], kt)], [("xt", xs)], "xl%d" % xs)
        if with_ma and (l, kt, "m") not in loaded:
            loaded.add((l, kt, "m"))
            dma(ma[ms][:], mT[:, kt * 128:(kt + 1) * 128].rearrange("(c p) t -> p c t", p=128),
                [("mT", c, j) for c in range(8)], [("ma", ms)], "ml%d" % ms)

    def x0_load(kt):
        if kt >= NT or (-1, kt) in loaded:
            return
        loaded.add((-1, kt))
        xs = kt % 4
        dma(xt[xs][:], Xbuf[0][kt * 128:(kt + 1) * 128, :], [("x", Xres[0], kt)], [("xt", xs)], "xl%d" % xs)

    def p3_tile(l, kt):
        last_layer = (l == nl - 1)
        xs, ms = kt % 4, kt % 3
        p3_loads(l, kt)
        p3_loads(l, kt + 1)
        p3_loads(l, kt + 2)
        for half in range(2):
            b = nextbank()
            for c in range(8):
                P.add("pe", lambda e, c=c, b=b, half=half: e.matmul(
                    ps[b][:, :], lhsT=ma[ms][:, c, :], rhs=w_out_sb[:, c, half * 512:(half + 1) * 512],
                    start=(c == 0), stop=(c == 7)),
                    reads=[("ma", ms), "w_out_sb"], writes=[("ps", b)])
            P.add("dve", lambda e, b=b, half=half: e.tensor_tensor(
                xt[xs][:, half * 512:(half + 1) * 512], ps[b][:, :], xt[xs][:, half * 512:(half + 1) * 512],
                ALU.add),
                reads=[("ps", b), ("xt", xs)], writes=[("xt", xs)])
        if last_layer and final:
            s = st1[kt % 2]
            sr = ("st1", kt % 2)
            P.add("act", lambda e: e.activation(hb[kt % 2][:], xt[xs][:], AF.Square, accum_out=s[:, 0:1]),
                  reads=[("xt", xs)], writes=[("hb", kt % 2), sr])
            P.add("act", lambda e: e.activation(s[:, 1:2], s[:, 0:1], AF.Ln, bias=EPS, scale=1.0 / D),
                  reads=[sr], writes=[sr])
            P.add("act", lambda e: e.activation(s[:, 2:3], s[:, 1:2], AF.Exp, scale=-0.5),
                  reads=[sr], writes=[sr])
            P.add("dve", lambda e: e.scalar_tensor_tensor(
                xt[xs][:], xt[xs][:], s[:, 2:3], fg_sb[:], ALU.mult, ALU.mult),
                reads=[("xt", xs), sr] + CONST, writes=[("xt", xs)])
        dma(Xbuf[l + 1][kt * 128:(kt + 1) * 128, :], xt[xs][:], [("xt", xs)], [("x", Xres[l + 1], kt)],
            "xs%d" % xs)

    for d_, c_ in load_w_in(0) + load_w_out(0) + load_pool_w(0):
        d_()
        c_()
    for l in range(nl):
        fr_ = [phase1_front(l, j) for j in range(NQ)]
        fronts = [list(f[0]) for f in fr_]
        for j in range(1, NQ):
            fronts[j].insert(3, fr_[j - 1][1])
        fms = [phase1_fm(l, j) for j in range(NQ)]
        fms[NQ - 1].append(fr_[NQ - 1][1])
        for st in fronts[0]:
            st()
        for j in range(NQ):
            a = fms[j]
            bsteps = fronts[j + 1] if j + 1 < NQ else []
            na, nb = len(a), len(bsteps)
            ib = 0
            for ia in range(na):
                a[ia]()
                tgt = ((ia + 1) * nb) // na
                while ib < tgt:
                    bsteps[ib]()
                    ib += 1
        extra = []
        if l + 1 < nl:
            extra = load_w_in(l + 1) + load_pool_w(l + 1)
        if l >= 1:
            extra = extra + load_w_out(l)
        phase2(l, extra)
    for kt in range(NT):
        p3_tile(nl - 1, kt)

    if DEBUG:
        dV = nc.dram_tensor("dbgV", [128, NT * H * 65], BF16, kind="ExternalOutput").ap()
        dC = nc.dram_tensor("dbgC", [128, NT * H], F32, kind="ExternalOutput").ap()
        dma(dV, V_flat[:, 0:NT * H * 65], [("V", kt) for kt in range(NT)], ["dbgV"], "c0")
        dma(dC, Call[:].rearrange("p a b -> p (a b)"), [("C", kt) for kt in range(NT)], ["dbgC"], "c1")
    P.emit(nc, stack)
    stack.close()
    return nc


_CACHE = {}


def _get_nc(nl, final):
    key = (nl, final)
    if key not in _CACHE:
        _CACHE[key] = build(nl, final)
    return _CACHE[key]


def _consts():
    bf = ml_dtypes.bfloat16
    ident = np.eye(128, dtype=np.float32)
    tri = np.triu(np.ones((128, 128), np.float32))
    kk = np.arange(128)[:, None]
    qq = np.arange(128)[None, :]
    maskn = np.where(qq < kk, -30000.0, 0.0).astype(np.float32)
    invc = np.zeros((128, 4, 16), np.float32)
    for g, w in enumerate((2, 4, 8, 16)):
        invc[:, g, :] = 1.0 / np.minimum(np.arange(16) + 1, w).astype(np.float32)
    return {"identb": ident.astype(bf), "identf": ident, "trif": tri,
            "onesf": np.ones((128, 128), np.float32), "maskn": maskn.astype(bf), "invc": invc}


def _launch(x, norm_g, w_in, forget_bias, pool_w, pool_scale, w_out, final_g, final):
    nl = w_in.shape[0]
    nc = _get_nc(nl, final)
    B = x.shape[0]
    common = dict(_consts())
    common["w_in"] = np.ascontiguousarray(w_in, dtype=np.float32)
    common["w_out"] = np.ascontiguousarray(w_out, dtype=np.float32)
    common["pool_w"] = np.ascontiguousarray(pool_w, dtype=np.float32)
    common["ng"] = np.ascontiguousarray(norm_g.reshape(nl, 8, 128).transpose(2, 0, 1), dtype=np.float32)
    common["psc"] = np.ascontiguousarray(pool_scale.reshape(nl, 4, 128).transpose(2, 0, 1), dtype=np.float32)
    common["fb"] = np.ascontiguousarray(np.broadcast_to(forget_bias[None], (128, nl, H)), dtype=np.float32)
    common["fg"] = np.ascontiguousarray(np.broadcast_to(final_g[None], (128, D)), dtype=np.float32)
    in_maps = []
    for b in range(B):
        m = dict(common)
        m["x"] = np.ascontiguousarray(x[b], dtype=np.float32)
        in_maps.append(m)
    res = run_bass_kernel_spmd(nc, in_maps, core_ids=list(range(B)))
    if DEBUG:
        global _DBG
        _DBG = res.results
    return np.stack([np.asarray(r["out"]) for r in res.results], axis=0).astype(np.float32)


def kernel(x, norm_g, w_in, forget_bias, pool_w, pool_scale, w_out, final_g):
    x = np.asarray(x)
    norm_g, w_in, forget_bias = np.asarray(norm_g), np.asarray(w_in), np.asarray(forget_bias)
    pool_w, pool_scale, w_out, final_g = np.asarray(pool_w), np.asarray(pool_scale), np.asarray(w_out), np.asarray(final_g)
    if MODE == "fused":
        return _launch(x, norm_g, w_in, forget_bias, pool_w, pool_scale, w_out, final_g, True)
    cur = x
    L = w_in.shape[0]
    for l in range(L):
        cur = _launch(cur, norm_g[l:l + 1], w_in[l:l + 1], forget_bias[l:l + 1], pool_w[l:l + 1],
                      pool_scale[l:l + 1], w_out[l:l + 1], final_g, l == L - 1)
    return cur
```

```python
import contextlib
import numpy as np
import ml_dtypes
import concourse.bass as bass
import concourse.mybir as mybir
from concourse.bass_utils import run_bass_kernel_spmd

F32 = mybir.dt.float32
BF16 = mybir.dt.bfloat16
AF = mybir.ActivationFunctionType
ALU = mybir.AluOpType

S = 4096
D = 1024
NT = 32
NQ = 8
H = 8
INC = 3080
OFF_PU, OFF_PG, OFF_Q, OFF_K, OFF_V, OFF_AG, OFF_F = 0, 512, 1024, 1536, 2048, 2560, 3072
DEPTH = 4
EPS = 1e-6
WCH = 1540

DEBUG = False
MODE = "fused"


class Prog:
    ENGS = ("pe", "act", "dve", "pool", "sp")

    def __init__(self):
        self.ops = {e: [] for e in self.ENGS}
        self.lastw = {}
        self.rds = {}
        self.dma_last = {}
        self.dma_cnt = {}

    def add(self, eng, fn, reads=(), writes=(), dma=None, extra_deps=()):
        deps = set(d for d in extra_deps if d is not None)
        for r in reads:
            t = self.lastw.get(r)
            if t is not None:
                deps.add(t)
        for w in writes:
            t = self.lastw.get(w)
            if t is not None:
                deps.add(t)
            for t in self.rds.get(w, {}).values():
                deps.add(t)
        idx = len(self.ops[eng])
        if dma is not None:
            n = self.dma_cnt.get(dma, 0) + 1
            self.dma_cnt[dma] = n
            tok = ("D", dma, n)
            prev = self.dma_last.get(dma)
            if prev is not None:
                deps.add(prev)
            self.dma_last[dma] = tok
        else:
            tok = ("E", eng, idx)
        best = {}
        for d in deps:
            k = (d[0], d[1])
            if k not in best or best[k][2] < d[2]:
                best[k] = d
        deps = set(best.values())
        self.ops[eng].append(dict(fn=fn, deps=deps, tok=tok, dma=dma, marked=False, semval=0))
        rk = (tok[0], tok[1])
        for r in reads:
            self.rds.setdefault(r, {})[rk] = tok
        for w in writes:
            self.lastw[w] = tok
            self.rds[w] = {}
        return tok

    def emit(self, nc, stack):
        for e in self.ENGS:
            for op in self.ops[e]:
                keep = set()
                for d in op["deps"]:
                    if d[0] == "E":
                        if d[1] == "pe" and e == "pe" and op["dma"] is None:
                            continue
                        self.ops[d[1]][d[2]]["marked"] = True
                    keep.add(d)
                op["deps"] = keep
        for e in self.ENGS:
            c = 0
            for op in self.ops[e]:
                if op["dma"] is None and op["marked"]:
                    c += 1
                    op["semval"] = c
        esem = {e: stack.enter_context(nc.semaphore("s_" + e)) for e in self.ENGS}
        dsem = {k: stack.enter_context(nc.semaphore("d_" + k)) for k in self.dma_cnt}

        def run(ename, eng):
            waited = {}
            for op in self.ops[ename]:
                for d in sorted(op["deps"]):
                    if d[0] == "E":
                        sem, key, val = esem[d[1]], "E" + d[1], self.ops[d[1]][d[2]]["semval"]
                    else:
                        sem, key, val = dsem[d[1]], "D" + d[1], 16 * d[2]
                    if waited.get(key, 0) < val:
                        eng.wait_ge(sem, val)
                        waited[key] = val
                ins = op["fn"](eng)
                if op["dma"] is not None:
                    ins.then_inc(dsem[op["dma"]], 16)
                elif op["marked"]:
                    ins.then_inc(esem[ename], 1)
            if ename == "sp":
                for k, n in self.dma_cnt.items():
                    if waited.get("D" + k, 0) < 16 * n:
                        eng.wait_ge(dsem[k], 16 * n)

        with nc.Block() as block:
            @block.tensor
            def _(e):
                run("pe", e)

            @block.scalar
            def _(e):
                run("act", e)

            @block.vector
            def _(e):
                run("dve", e)

            @block.gpsimd
            def _(e):
                run("pool", e)

            @block.sync
            def _(e):
                run("sp", e)


def build(nl, final):
    nc = bass.Bass("TRN2", target_bir_lowering=False)
    stack = contextlib.ExitStack()
    P = Prog()

    def din(name, shape, dt):
        return nc.dram_tensor(name, shape, dt, kind="ExternalInput").ap()

    def dscr(name, shape, dt):
        return nc.dram_tensor(name, shape, dt, kind="ExternalOutput" if DEBUG else "Internal").ap()

    x_in = din("x", [S, D], F32)
    w_in = din("w_in", [nl, D, INC], F32)
    w_out = din("w_out", [nl, D, D], F32)
    pool_w = din("pool_w", [nl, 4, 128, 128], F32)
    ng = din("ng", [128, nl, 8], F32)
    psc = din("psc", [128, nl, 4], F32)
    fb = din("fb", [128, nl, 8], F32)
    fg = din("fg", [128, D], F32)
    c_identb = din("identb", [128, 128], BF16)
    c_identf = din("identf", [128, 128], F32)
    c_trif = din("trif", [128, 128], F32)
    c_onesf = din("onesf", [128, 128], F32)
    c_maskn = din("maskn", [128, 128], BF16)
    c_invc = din("invc", [128, 4, 16], F32)
    out = nc.dram_tensor("out", [S, D], F32, kind="ExternalOutput").ap()

    xa = dscr("xa", [S, D], F32)
    xb = dscr("xb", [S, D], F32)
    qT = dscr("qT", [512, S], BF16)
    kT = dscr("kT", [512, S], BF16)
    gT = dscr("gT", [512, S], F32)
    mT = dscr("mT", [1024, S], BF16)
    crow = dscr("crow", [NT, H, 128], BF16)

    def sb(name, shape, dt):
        return stack.enter_context(nc.sbuf_tensor(name, shape, dt))

    w_in_sb = sb("w_in_sb", [128, 8, INC], BF16)
    w_out_sb = sb("w_out_sb", [128, 8, D], BF16)
    poolw_sb = sb("poolw_sb", [128, 4, 128], BF16)
    V_flat = sb("V_all", [128, NT * H * 65 + 64], BF16)
    V_all = V_flat[:, 0:NT * H * 65].rearrange("p (a b c) -> p a b c", a=NT, b=H)
    Call = sb("Call", [128, NT, H], F32)
    identb = sb("identb_sb", [128, 128], BF16)
    identf = sb("identf_sb", [128, 128], F32)
    trif = sb("trif_sb", [128, 128], F32)
    onesf = sb("onesf_sb", [128, 128], F32)
    maskn = sb("maskn_sb", [128, 128], BF16)
    invc = sb("invc_sb", [128, 4, 16], F32)
    ng_sb = sb("ng_sb", [128, nl, 8], F32)
    psc_sb = sb("psc_sb", [128, nl, 4], F32)
    fb_sb = sb("fb_sb", [128, nl, 8], F32)
    fg_sb = sb("fg_sb", [128, D], F32)
    stage = sb("stage", [128, WCH], F32)
    big = [sb("big%d" % i, [128, 4096], BF16) for i in range(2)]
    xt = [sb("xt%d" % i, [128, D], F32) for i in range(4)]
    hb = [sb("hb%d" % i, [128, D], BF16) for i in range(2)]
    ubuf = sb("ubuf", [128, 4, 528], F32)
    ptmp = [sb("ptmp%d" % i, [128, 528], F32) for i in range(2)]
    dd = [sb("dd%d" % i, [128, 512], BF16) for i in range(2)]
    ge = [sb("ge%d" % i, [128, 512], F32) for i in range(2)]
    sg = [sb("sg%d" % i, [128, 512], F32) for i in range(2)]
    qk_sb = [sb("qk%d" % i, [128, 512], BF16) for i in range(3)]
    mp_sb = [sb("mp%d" % i, [128, 512], BF16) for i in range(2)]
    st1 = [sb("st1_%d" % i, [128, 4], F32) for i in range(2)]
    fz = [sb("fz%d" % i, [128, 3, 8], F32) for i in range(2)]
    carry = [sb("carry%d" % i, [128, 8], F32) for i in range(2)]
    cneg = sb("cneg", [128, 4, 8], F32)
    crT = sb("crT", [32, 128], BF16)
    QTa = [sb("QTa%d" % i, [128, 512], BF16) for i in range(2)]
    ga = [sb("ga%d" % i, [64, 512], F32) for i in range(4)]
    Pt = [sb("Pt%d" % i, [128, 512], BF16) for i in range(4)]
    Tsb = [sb("Tsb%d" % i, [65, 512], F32) for i in range(2)]
    srow = [sb("srow%d" % i, [128, 1024], BF16) for i in range(2)]
    sel = sb("sel", [128, 128], BF16)
    msb = [sb("msb%d" % i, [64, 512], BF16) for i in range(2)]
    ma = [sb("ma%d" % i, [128, 8, 128], BF16) for i in range(2)]

    ps = [stack.enter_context(nc.psum_tensor("ps%d" % i, [128, 512], F32)) for i in range(8)]

    rr = {"b": 0, "st": 0}

    def nextbank(n=8):
        b = rr["b"] % n
        rr["b"] += 1
        return b

    def stkey():
        rr["st"] += 1
        return "st%d" % (rr["st"] % 8)

    def dma(out_ap, in_ap, reads, writes, key):
        q = "pool" if (key.startswith("st") or key.startswith("xs") or key.startswith("wst")) else "sp"
        P.add(q, lambda e: e.dma_start(out=out_ap, in_=in_ap), reads=reads, writes=writes, dma=key)

    for i, (dst, src) in enumerate([(identb, c_identb), (identf, c_identf), (trif, c_trif), (onesf, c_onesf),
                                    (maskn, c_maskn), (invc, c_invc), (ng_sb, ng), (psc_sb, psc),
                                    (fb_sb, fb), (fg_sb, fg)]):
        dma(dst[:], src, [], [("c", i)], "c%d" % (i % 2))
    CONST = [("c", i) for i in range(10)]
    P.add("pool", lambda e: e.memset(V_flat[:, NT * H * 65:], 0.0), writes=[("V", kt) for kt in range(NT)])
    P.add("pool", lambda e: e.memset(V_all[:, :, :, 64:65], 1.0), writes=[("V", kt) for kt in range(NT)])
    P.add("pool", lambda e: e.memset(sel[:], 0.0), writes=["sel"])
    P.add("pool", lambda e: e.memset(sel[64:65, 0:64], 1.0), writes=["sel"])
    for i in range(2):
        P.add("pool", lambda e, i=i: e.memset(srow[i][:], 0.0), writes=[("srow", i)])

    Xbuf = [x_in] + [xa if (l % 2 == 1) else xb for l in range(1, nl)] + [out]
    Xres = ["xin"] + ["xa" if (l % 2 == 1) else "xb" for l in range(1, nl)] + ["xout"]

    def load_w_in(l):
        steps = []
        for kc in range(8):
            for cc in range(INC // WCH):
                c0 = cc * WCH

                def d(kc=kc, c0=c0):
                    dma(stage[:, 0:WCH], w_in[l, kc * 128:(kc + 1) * 128, c0:c0 + WCH], [], ["stage"], "wst")

                def c(kc=kc, c0=c0):
                    P.add("dve", lambda e: e.tensor_scalar(w_in_sb[:, kc, c0:c0 + WCH], stage[:, 0:WCH],
                                                           ng_sb[:, l, kc:kc + 1], None, ALU.mult),
                          reads=["stage"] + CONST, writes=["w_in_sb"])
                steps.append((d, c))
        return steps

    def load_w_out(l):
        steps = []
        for kc in range(8):
            def d(kc=kc):
                dma(stage[:, 0:D], w_out[l, kc * 128:(kc + 1) * 128, :], [], ["stage"], "wst")

            def c(kc=kc):
                P.add("dve", lambda e: e.tensor_copy(w_out_sb[:, kc, :], stage[:, 0:D]),
                      reads=["stage"], writes=["w_out_sb"])
            steps.append((d, c))
        return steps

    def load_pool_w(l):
        def d():
            dma(stage[:, 0:512].rearrange("p (g c) -> p g c", g=4), pool_w[l].rearrange("g p c -> p g c"),
                [], ["stage"], "wst")

        def c():
            P.add("dve", lambda e: e.tensor_copy(poolw_sb[:].rearrange("p g c -> p (g c)"), stage[:, 0:512]),
                  reads=["stage"], writes=["poolw_sb"])
        return [(d, c)]

    def phase1_front(l, j):
        steps = []
        slot = j % 2
        hT = big[slot]
        xsrc = Xbuf[l]

        for sub in range(4):
            kt = j * 4 + sub
            xs = kt % 4
            hs = kt % 2

            def s_norm(kt=kt, xs=xs, hs=hs):
                if l == 0:
                    x0_load(kt)
                    x0_load(kt + 1)
                    x0_load(kt + 2)
                else:
                    p3_tile(l - 1, kt)
                s = st1[hs]
                P.add("act", lambda e: e.activation(hb[hs][:], xt[xs][:], AF.Square, accum_out=s[:, 0:1]),
                      reads=[("xt", xs)], writes=[("hb", hs), ("st1", hs)])
                P.add("act", lambda e: e.activation(s[:, 1:2], s[:, 0:1], AF.Ln, bias=EPS, scale=1.0 / D),
                      reads=[("st1", hs)], writes=[("st1", hs)])
                P.add("act", lambda e: e.activation(s[:, 2:3], s[:, 1:2], AF.Exp, scale=-0.5),
                      reads=[("st1", hs)], writes=[("st1", hs)])
                P.add("act", lambda e: e.activation(hb[hs][:], xt[xs][:], AF.Copy, scale=s[:, 2:3]),
                      reads=[("xt", xs), ("st1", hs)], writes=[("hb", hs)])
            steps.append(s_norm)

            def s_tr(kt=kt, sub=sub, hs=hs):
                for half in range(2):
                    b = nextbank()
                    for q in range(4):
                        kc = half * 4 + q
                        P.add("pe", lambda e, kc=kc, q=q, b=b: e.matmul(
                            ps[b][:, q * 128:(q + 1) * 128], lhsT=hb[hs][:, kc * 128:(kc + 1) * 128],
                            rhs=identb[:], start=True, stop=True),
                            reads=[("hb", hs)] + CONST, writes=[("ps", b)])
                    P.add("dve", lambda e, b=b, half=half: e.tensor_copy(
                        hT[:].rearrange("p (k t) -> p k t", k=8)[:, half * 4:half * 4 + 4, sub * 128:(sub + 1) * 128],
                        ps[b][:].rearrange("p (k t) -> p k t", k=4)),
                        reads=[("ps", b)], writes=[("big", slot, sub)])
            steps.append(s_tr)

            def s_vf(kt=kt, sub=sub):
                hT3 = hT[:].rearrange("p (k t) -> p k t", k=8)
                b = nextbank()
                for kc in range(8):
                    P.add("pe", lambda e, kc=kc, b=b: e.matmul(
                        ps[b][:, :], lhsT=hT3[:, kc, sub * 128:(sub + 1) * 128],
                        rhs=w_in_sb[:, kc, OFF_V:OFF_V + 512], start=(kc == 0), stop=(kc == 7)),
                        reads=[("big", slot, sub), "w_in_sb"], writes=[("ps", b)])
                P.add("act", lambda e, b=b: e.activation(
                    V_all[:, kt, :, 0:64], ps[b][:].rearrange("p (h d) -> p h d", h=8), AF.Copy),
                    reads=[("ps", b)], writes=[("V", kt)])
                b2 = nextbank()
                for kc in range(8):
                    P.add("pe", lambda e, kc=kc, b2=b2: e.matmul(
                        ps[b2][:, 0:8], lhsT=hT3[:, kc, sub * 128:(sub + 1) * 128],
                        rhs=w_in_sb[:, kc, OFF_F:OFF_F + 8], start=(kc == 0), stop=(kc == 7)),
                        reads=[("big", slot, sub), "w_in_sb"], writes=[("ps", b2)])
                f = fz[kt % 2]
                fr = ("fz", kt % 2)
                P.add("dve", lambda e: e.tensor_tensor(f[:, 0, :], ps[b2][:, 0:8], fb_sb[:, l, :], ALU.add),
                      reads=[("ps", b2)] + CONST, writes=[fr])
                P.add("act", lambda e: e.activation(f[:, 1, :], f[:, 0, :], AF.Exp, scale=-1.0),
                      reads=[fr], writes=[fr])
                P.add("act", lambda e: e.activation(f[:, 2, :], f[:, 1, :], AF.Ln, bias=1.0, scale=1.0),
                      reads=[fr], writes=[fr])
            steps.append(s_vf)

            def s_cum(kt=kt, sub=sub):
                f = fz[kt % 2]
                fr = ("fz", kt % 2)
                b3 = nextbank()
                P.add("pe", lambda e: e.matmul(ps[b3][:, 0:8], lhsT=trif[:], rhs=f[:, 2, :], start=True, stop=True),
                      reads=[fr] + CONST, writes=[("ps", b3)])
                P.add("pe", lambda e: e.matmul(ps[b3][:, 8:16], lhsT=onesf[:], rhs=f[:, 2, :], start=True, stop=True),
                      reads=[fr] + CONST, writes=[("ps", b3)])
                cin, cout = carry[kt % 2], carry[(kt + 1) % 2]
                if kt == 0:
                    P.add("dve", lambda e: e.tensor_copy(Call[:, kt, :], ps[b3][:, 0:8]),
                          reads=[("ps", b3)], writes=[("C", kt)])
                    P.add("dve", lambda e: e.tensor_copy(cout[:], ps[b3][:, 8:16]),
                          reads=[("ps", b3)], writes=[("carry", (kt + 1) % 2)])
                else:
                    P.add("dve", lambda e: e.tensor_tensor(Call[:, kt, :], ps[b3][:, 0:8], cin[:], ALU.add),
                          reads=[("ps", b3), ("carry", kt % 2)], writes=[("C", kt)])
                    P.add("dve", lambda e: e.tensor_tensor(cout[:], ps[b3][:, 8:16], cin[:], ALU.add),
                          reads=[("ps", b3), ("carry", kt % 2)], writes=[("carry", (kt + 1) % 2)])
                P.add("dve", lambda e: e.tensor_scalar(cneg[:, sub, :], Call[:, kt, :], -8.0, None, ALU.mult),
                      reads=[("C", kt)], writes=["cneg"])
            steps.append(s_cum)

        def s_crow():
            b = nextbank()
            P.add("pe", lambda e: e.transpose(ps[b][0:32, 0:128], cneg[:].rearrange("p s h -> p (s h)"), identf[:]),
                  reads=["cneg"] + CONST, writes=[("ps", b)])
            P.add("act", lambda e: e.activation(crT[:], ps[b][0:32, 0:128], AF.Copy),
                  reads=[("ps", b)], writes=["crT"])
            dma(crow[j * 4:(j + 1) * 4].rearrange("s h t -> (s h) t"), crT[:], ["crT"], [("cr", j)], stkey())
        steps.append(s_crow)
        n_, t_, v_, c_ = ([steps[4 * i + k] for i in range(4)] for k in range(4))
        order = [n_[0], n_[1], t_[0], n_[2], t_[1], v_[0], n_[3], t_[2], v_[1], c_[0], t_[3], v_[2], c_[1],
                 v_[3], c_[2], c_[3]]
        return order, steps[16]

    def phase1_fm(l, j):
        steps = []
        slot = j % 2
        hT3 = big[slot][:].rearrange("p (k t) -> p k t", k=8)
        hres = [("big", slot, s) for s in range(4)]
        cols = slice(j * 512, (j + 1) * 512)

        def proj(c0):
            b = nextbank()
            for kc in range(8):
                P.add("pe", lambda e, kc=kc: e.matmul(
                    ps[b][:, :], lhsT=w_in_sb[:, kc, c0:c0 + 128], rhs=hT3[:, kc, :],
                    start=(kc == 0), stop=(kc == 7)),
                    reads=hres + ["w_in_sb"], writes=[("ps", b)])
            return b

        def gate(b, gs):
            P.add("act", lambda e: e.activation(ge[gs][:], ps[b][:, :], AF.Exp, scale=-1.0),
                  reads=[("ps", b)], writes=[("ge", gs)])
            P.add("act", lambda e: e.activation(ge[gs][:], ge[gs][:], AF.Ln, bias=1.0, scale=1.0),
                  reads=[("ge", gs)], writes=[("ge", gs)])
            P.add("act", lambda e: e.activation(ge[gs][:], ge[gs][:], AF.Exp, scale=-1.0),
                  reads=[("ge", gs)], writes=[("ge", gs)])
            P.add("dve", lambda e: e.tensor_tensor(sg[gs][:], ps[b][:, :], ge[gs][:], ALU.mult),
                  reads=[("ps", b), ("ge", gs)], writes=[("sg", gs)])

        pa, pb, qs_ = [], [], []
        for g in range(4):
            def s_pool(g=g):
                w = 2 << g
                gs = g % 2
                ur = ("u", g)
                if j == 0:
                    P.add("pool", lambda e: e.memset(ubuf[:, g, 0:16], 0.0), writes=[ur])
                b = proj(OFF_PU + g * 128)
                P.add("act", lambda e: e.activation(ubuf[:, g, 16:528], ps[b][:, :], AF.Copy),
                      reads=[("ps", b)], writes=[ur])
                b2 = proj(OFF_PG + g * 128)
                gate(b2, gs)
                src = ubuf[:, g, :]
                srcr = ur
                sh = 1
                k = 0
                while sh < w:
                    dst = ptmp[k % 2]
                    dr = ("ptmp", k % 2)
                    lo = 2 * sh - 1
                    P.add("dve", lambda e, src=src, dst=dst, sh=sh, lo=lo: e.tensor_tensor(
                        dst[:, lo:528], src[:, lo:528], src[:, lo - sh:528 - sh], ALU.add),
                        reads=[srcr], writes=[dr])
                    src, srcr = dst[:, :], dr
                    sh *= 2
                    k += 1
                ds = ("dd", gs)
                P.add("dve", lambda e, src=src: e.scalar_tensor_tensor(
                    dd[gs][:], src[:, 16:528], 1.0 / w, ubuf[:, g, 16:528], ALU.mult, ALU.subtract),
                    reads=[srcr, ur], writes=[ds])
                if j == 0:
                    P.add("dve", lambda e, src=src: e.tensor_tensor(
                        ptmp[k % 2][:, 0:16], src[:, 16:32], invc[:, g, :], ALU.mult),
                        reads=[srcr] + CONST, writes=[("ptmp", k % 2)])
                    P.add("dve", lambda e: e.tensor_tensor(
                        dd[gs][:, 0:16], ptmp[k % 2][:, 0:16], ubuf[:, g, 16:32], ALU.subtract),
                        reads=[("ptmp", k % 2), ur], writes=[ds])
                P.add("dve", lambda e: e.tensor_copy(ubuf[:, g, 0:16], ubuf[:, g, 512:528]),
                      reads=[ur, ("ptmp", 0), ("ptmp", 1), ds], writes=[ur])
            pa.append(s_pool)

            def s_poolb(g=g):
                gs = g % 2
                ds = ("dd", gs)
                b3 = nextbank()
                P.add("pe", lambda e: e.matmul(ps[b3][:, :], lhsT=poolw_sb[:, g, :], rhs=dd[gs][:],
                                               start=True, stop=True),
                      reads=[ds, "poolw_sb"], writes=[("ps", b3)])
                P.add("dve", lambda e: e.scalar_tensor_tensor(
                    mp_sb[gs][:], ps[b3][:, :], psc_sb[:, l, g:g + 1], sg[gs][:], ALU.mult, ALU.mult),
                    reads=[("ps", b3), ("sg", gs)] + CONST, writes=[("mp", gs)])
                dma(mT[g * 128:(g + 1) * 128, cols], mp_sb[gs][:], [("mp", gs)], [("mT", g, j)], stkey())
            pb.append(s_poolb)

        for p in range(4):
            def s_q(p=p):
                for which, off, dst, nm in ((0, OFF_Q, qT, "qT"), (1, OFF_K, kT, "kT")):
                    b = proj(off + p * 128)
                    qs = (2 * p + which) % 3
                    if which == 0:
                        P.add("dve", lambda e, b=b, qs=qs: e.tensor_copy(qk_sb[qs][:], ps[b][:, :]),
                              reads=[("ps", b)], writes=[("qk", qs)])
                    else:
                        P.add("act", lambda e, b=b, qs=qs: e.activation(qk_sb[qs][:], ps[b][:, :], AF.Copy),
                              reads=[("ps", b)], writes=[("qk", qs)])
                    dma(dst[p * 128:(p + 1) * 128, cols], qk_sb[qs][:], [("qk", qs)], [(nm, p, j)], stkey())
                b = proj(OFF_AG + p * 128)
                gs = p % 2
                gate(b, gs)
                dma(gT[p * 128:(p + 1) * 128, cols], sg[gs][:], [("sg", gs)], [("gT", p, j)], stkey())
            qs_.append(s_q)
        return [pa[0], pa[1], pb[0], pa[2], pb[1], pa[3], pb[2], qs_[0], pb[3], qs_[1], qs_[2], qs_[3]]

    def phase2(l, extra_steps):
        extra = list(extra_steps)
        blocks = []
        for h in range(H):
            for j in range(NQ):
                for kt in range(4 * j + 4):
                    blocks.append((h, j, kt))
        LOOK = 3
        DEFER = 5
        pending = []
        exp_tok = {}
        castq = []

        def load_k(h):
            ks = h % 2
            kres = [("big", ks, s) for s in range(4)]
            dma(big[ks][0:64, :], kT[h * 64:(h + 1) * 64, :], [("kT", h // 2, jj) for jj in range(NQ)], kres,
                "kl%d" % ks)
            P.add("pool", lambda e: e.memset(big[ks][64:65, :], 1.0), writes=kres)

        def load_q(h, j):
            qi = h * NQ + j
            qs = qi % 2
            cols = slice(j * 512, (j + 1) * 512)
            dma(QTa[qs][0:64, :], qT[h * 64:(h + 1) * 64, cols], [("qT", h // 2, j)], [("QTa", qs)], "ql%d" % qs)
            dma(QTa[qs][64:65, :].rearrange("p (s t) -> p s t", s=4),
                crow[j * 4:(j + 1) * 4, h:h + 1, :].rearrange("s o t -> o s t"),
                [("cr", j)], [("QTc", qs)], "cl%d" % qs)
            gs3 = qi % 4
            dma(ga[gs3][:], gT[h * 64:(h + 1) * 64, cols], [("gT", h // 2, j)], [("ga", gs3)], "gl%d" % gs3)

        def emit_S(i):
            h, j, kt = blocks[i]
            qs = (h * NQ + j) % 2
            ks = h % 2
            b = i % 4
            diag = kt >= 4 * j
            c0 = 128 * (kt - 4 * j) if diag else 0
            P.add("pe", lambda e: e.matmul(ps[b][:, c0:512], lhsT=big[ks][0:65, kt * 128:(kt + 1) * 128],
                                           rhs=QTa[qs][0:65, c0:512], start=True, stop=not diag),
                  reads=[("big", ks, 0), ("QTa", qs), ("QTc", qs)], writes=[("ps", b)],
                  extra_deps=[exp_tok.get(i - LOOK)])
            if diag:
                P.add("pe", lambda e: e.matmul(ps[b][:, c0:c0 + 128], lhsT=identb[:], rhs=maskn[:],
                                               start=False, stop=True),
                      reads=CONST, writes=[("ps", b)])
            exp_tok[i] = P.add("act", lambda e: e.activation(Pt[b][:, c0:512], ps[b][:, c0:512], AF.Exp,
                                                             bias=Call[:, kt, h:h + 1], scale=0.125),
                               reads=[("ps", b), ("C", kt)], writes=[("Pt", b)])
            if DEBUG and i == 0:
                dS = nc.dram_tensor("dbgS", [128, 512], F32, kind="ExternalOutput").ap()
                dP = nc.dram_tensor("dbgP", [128, 512], BF16, kind="ExternalOutput").ap()
                dK = nc.dram_tensor("dbgK", [65, 512], BF16, kind="ExternalOutput").ap()
                dQ = nc.dram_tensor("dbgQ", [65, 512], BF16, kind="ExternalOutput").ap()
                P.add("act", lambda e: e.activation(xt[0][:, 0:512], ps[b][:, :], AF.Copy),
                      reads=[("ps", b)], writes=[("xt", 0)])
                dma(dS, xt[0][:, 0:512], [("xt", 0)], ["dbgS"], "c0")
                dma(dP, Pt[b][:, :], [("Pt", b)], ["dbgP"], "c0")
                dma(dK, big[ks][0:65, 0:512], [("big", ks, 0)], ["dbgK"], "c0")
                dma(dQ, QTa[qs][0:65, :], [("QTa", qs), ("QTc", qs)], ["dbgQ"], "c0")

        def emit_PV(i):
            h, j, kt = blocks[i]
            b = i % 4
            qi = h * NQ + j
            ob = (4, 5, 7)[qi % 3]
            qs = qi % 2
            diag = kt >= 4 * j
            c0 = 128 * (kt - 4 * j) if diag else 0
            last = kt == 4 * j + 3
            v0 = (kt * H + h) * 65
            P.add("pe", lambda e: e.matmul(ps[ob][:, c0:512], lhsT=V_flat[:, v0:v0 + 128], rhs=Pt[b][:, c0:512],
                                           start=(kt == 0), stop=last),
                  reads=[("Pt", b), ("V", kt), ("V", min(kt + 1, NT - 1))], writes=[("ps", ob)])
            if last:
                cols = slice(j * 512, (j + 1) * 512)
                es = qi % 2
                T_, sr_ = Tsb[es], srow[es]
                P.add("dve", lambda e: e.tensor_copy(sr_[64:65, 0:512], ps[ob][64:65, :]),
                      reads=[("ps", ob)], writes=[("srow", es)])
                P.add("dve", lambda e: e.tensor_tensor(sr_[64:65, 512:1024], ps[ob][64:65, :], sr_[64:65, 0:512],
                                                       ALU.subtract),
                      reads=[("ps", ob), ("srow", es)], writes=[("srow", es)])

                def part2():
                    P.add("pe", lambda e: e.matmul(ps[6][:, :], lhsT=sel[:], rhs=sr_[:, 0:512],
                                                   start=True, stop=False),
                          reads=[("srow", es), "sel"], writes=[("ps", 6)])
                    P.add("pe", lambda e: e.matmul(ps[6][:, :], lhsT=sel[:], rhs=sr_[:, 512:1024],
                                                   start=False, stop=True),
                          reads=[("srow", es), "sel"], writes=[("ps", 6)])
                    gs3 = qi % 4
                    P.add("dve", lambda e: e.reciprocal(T_[0:64, :], ps[6][0:64, :]),
                          reads=[("ps", 6)], writes=[("Tsb", es)])
                    P.add("dve", lambda e: e.tensor_tensor(T_[0:64, :], T_[0:64, :], ga[gs3][:], ALU.mult),
                          reads=[("Tsb", es), ("ga", gs3)], writes=[("Tsb", es)])
                    P.add("dve", lambda e: e.tensor_tensor(msb[qs][:], ps[ob][0:64, :], T_[0:64, :], ALU.mult),
                          reads=[("ps", ob), ("Tsb", es)], writes=[("msb", qs)])
                    dma(mT[512 + h * 64:512 + (h + 1) * 64, cols], msb[qs][:], [("msb", qs)],
                        [("mT", 4 + h // 2, j)], stkey())
                pending.append((i + LOOK + DEFER, part2))

        load_k(0)
        load_q(0, 0)
        n = len(blocks)
        for i in range(n + LOOK):
            while pending and pending[0][0] <= i:
                pending.pop(0)[1]()
            if i < n:
                h, j, kt = blocks[i]
                if kt == 0:
                    nxt = h * NQ + j + 1
                    if nxt < H * NQ:
                        nh, nj = divmod(nxt, NQ)
                        if nj == 0:
                            load_k(nh)
                        load_q(nh, nj)
                    if j >= 2:
                        if castq:
                            castq.pop(0)()
                        if extra:
                            d_, c_ = extra.pop(0)
                            d_()
                            castq.append(c_)
                emit_S(i)
            if i - LOOK >= 0:
                emit_PV(i - LOOK)
        while pending:
            pending.pop(0)[1]()
        while castq or extra:
            if castq:
                castq.pop(0)()
            if extra:
                d_, c_ = extra.pop(0)
                d_()
                castq.append(c_)

    loaded = set()

    def p3_loads(l, kt, with_ma=True):
        if kt >= NT:
            return
        xs, ms, j = kt % 4, kt % 2, kt // 4
        if (l, kt) not in loaded:
            loaded.add((l, kt))
            dma(xt[xs][:], Xbuf[l][kt * 128:(kt + 1) * 128, :], [("x", Xres[l], kt)], [("xt", xs)], "xl%d" % xs)
        if with_ma and (l, kt, "m") not in loaded:
            loaded.add((l, kt, "m"))
            dma(ma[ms][:], mT[:, kt * 128:(kt + 1) * 128].rearrange("(c p) t -> p c t", p=128),
                [("mT", c, j) for c in range(8)], [("ma", ms)], "ml%d" % ms)

    def x0_load(kt):
        if kt >= NT or (-1, kt) in loaded:
            return
        loaded.add((-1, kt))
        xs = kt % 4
        dma(xt[xs][:], Xbuf[0][kt * 128:(kt + 1) * 128, :], [("x", Xres[0], kt)], [("xt", xs)], "xl%d" % xs)

    def p3_tile(l, kt):
        last_layer = (l == nl - 1)
        xs, ms = kt % 4, kt % 2
        p3_loads(l, kt)
        p3_loads(l, kt + 1)
        p3_loads(l, kt + 2, with_ma=False)
        for half in range(2):
            b = nextbank()
            for c in range(8):
                P.add("pe", lambda e, c=c, b=b, half=half: e.matmul(
                    ps[b][:, :], lhsT=ma[ms][:, c, :], rhs=w_out_sb[:, c, half * 512:(half + 1) * 512],
                    start=(c == 0), stop=(c == 7)),
                    reads=[("ma", ms), "w_out_sb"], writes=[("ps", b)])
            P.add("dve", lambda e, b=b, half=half: e.tensor_tensor(
                xt[xs][:, half * 512:(half + 1) * 512], ps[b][:, :], xt[xs][:, half * 512:(half + 1) * 512],
                ALU.add),
                reads=[("ps", b), ("xt", xs)], writes=[("xt", xs)])
        if last_layer and final:
            s = st1[kt % 2]
            sr = ("st1", kt % 2)
            P.add("act", lambda e: e.activation(hb[kt % 2][:], xt[xs][:], AF.Square, accum_out=s[:, 0:1]),
                  reads=[("xt", xs)], writes=[("hb", kt % 2), sr])
            P.add("act", lambda e: e.activation(s[:, 1:2], s[:, 0:1], AF.Ln, bias=EPS, scale=1.0 / D),
                  reads=[sr], writes=[sr])
            P.add("act", lambda e: e.activation(s[:, 2:3], s[:, 1:2], AF.Exp, scale=-0.5),
                  reads=[sr], writes=[sr])
            P.add("dve", lambda e: e.scalar_tensor_tensor(
                xt[xs][:], xt[xs][:], s[:, 2:3], fg_sb[:], ALU.mult, ALU.mult),
                reads=[("xt", xs), sr] + CONST, writes=[("xt", xs)])
        dma(Xbuf[l + 1][kt * 128:(kt + 1) * 128, :], xt[xs][:], [("xt", xs)], [("x", Xres[l + 1], kt)],
            "xs%d" % xs)

    for d_, c_ in load_w_in(0) + load_w_out(0) + load_pool_w(0):
        d_()
        c_()
    for l in range(nl):
        fr_ = [phase1_front(l, j) for j in range(NQ)]
        fronts = [list(f[0]) for f in fr_]
        for j in range(1, NQ):
            fronts[j].insert(3, fr_[j - 1][1])
        fms = [phase1_fm(l, j) for j in range(NQ)]
        fms[NQ - 1].append(fr_[NQ - 1][1])
        for st in fronts[0]:
            st()
        for j in range(NQ):
            a = fms[j]
            bsteps = fronts[j + 1] if j + 1 < NQ else []
            na, nb = len(a), len(bsteps)
            ib = 0
            for ia in range(na):
                a[ia]()
                tgt = ((ia + 1) * nb) // na
                while ib < tgt:
                    bsteps[ib]()
                    ib += 1
        extra = []
        if l + 1 < nl:
            extra = load_w_in(l + 1) + load_pool_w(l + 1)
        if l >= 1:
            extra = extra + load_w_out(l)
        phase2(l, extra)
    for kt in range(NT):
        p3_tile(nl - 1, kt)

    if DEBUG:
        dV = nc.dram_tensor("dbgV", [128, NT * H * 65], BF16, kind="ExternalOutput").ap()
        dC = nc.dram_tensor("dbgC", [128, NT * H], F32, kind="ExternalOutput").ap()
        dma(dV, V_flat[:, 0:NT * H * 65], [("V", kt) for kt in range(NT)], ["dbgV"], "c0")
        dma(dC, Call[:].rearrange("p a b -> p (a b)"), [("C", kt) for kt in range(NT)], ["dbgC"], "c1")
    P.emit(nc, stack)
    stack.close()
    return nc


_CACHE = {}


def _get_nc(nl, final):
    key = (nl, final)
    if key not in _CACHE:
        _CACHE[key] = build(nl, final)
    return _CACHE[key]


def _consts():
    bf = ml_dtypes.bfloat16
    ident = np.eye(128, dtype=np.float32)
    tri = np.triu(np.ones((128, 128), np.float32))
    kk = np.arange(128)[:, None]
    qq = np.arange(128)[None, :]
    maskn = np.where(qq < kk, -30000.0, 0.0).astype(np.float32)
    invc = np.zeros((128, 4, 16), np.float32)
    for g, w in enumerate((2, 4, 8, 16)):
        invc[:, g, :] = 1.0 / np.minimum(np.arange(16) + 1, w).astype(np.float32)
    return {"identb": ident.astype(bf), "identf": ident, "trif": tri,
            "onesf": np.ones((128, 128), np.float32), "maskn": maskn.astype(bf), "invc": invc}


def _launch(x, norm_g, w_in, forget_bias, pool_w, pool_scale, w_out, final_g, final):
    nl = w_in.shape[0]
    nc = _get_nc(nl, final)
    B = x.shape[0]
    common = dict(_consts())
    common["w_in"] = np.ascontiguousarray(w_in, dtype=np.float32)
    common["w_out"] = np.ascontiguousarray(w_out, dtype=np.float32)
    common["pool_w"] = np.ascontiguousarray(pool_w, dtype=np.float32)
    common["ng"] = np.ascontiguousarray(norm_g.reshape(nl, 8, 128).transpose(2, 0, 1), dtype=np.float32)
    common["psc"] = np.ascontiguousarray(pool_scale.reshape(nl, 4, 128).transpose(2, 0, 1), dtype=np.float32)
    common["fb"] = np.ascontiguousarray(np.broadcast_to(forget_bias[None], (128, nl, H)), dtype=np.float32)
    common["fg"] = np.ascontiguousarray(np.broadcast_to(final_g[None], (128, D)), dtype=np.float32)
    in_maps = []
    for b in range(B):
        m = dict(common)
        m["x"] = np.ascontiguousarray(x[b], dtype=np.float32)
        in_maps.append(m)
    res = run_bass_kernel_spmd(nc, in_maps, core_ids=list(range(B)))
    if DEBUG:
        global _DBG
        _DBG = res.results
    return np.stack([np.asarray(r["out"]) for r in res.results], axis=0).astype(np.float32)


def kernel(x, norm_g, w_in, forget_bias, pool_w, pool_scale, w_out, final_g):
    x = np.asarray(x)
    norm_g, w_in, forget_bias = np.asarray(norm_g), np.asarray(w_in), np.asarray(forget_bias)
    pool_w, pool_scale, w_out, final_g = np.asarray(pool_w), np.asarray(pool_scale), np.asarray(w_out), np.asarray(final_g)
    if MODE == "fused":
        return _launch(x, norm_g, w_in, forget_bias, pool_w, pool_scale, w_out, final_g, True)
    cur = x
    L = w_in.shape[0]
    for l in range(L):
        cur = _launch(cur, norm_g[l:l + 1], w_in[l:l + 1], forget_bias[l:l + 1], pool_w[l:l + 1],
                      pool_scale[l:l + 1], w_out[l:l + 1], final_g, l == L - 1)
    return cur
```
